# Optimizing a Trainium2 kernel written in Bass

```python
import jax, jax.numpy as jnp
from jax import lax
import numpy as np

D_MODEL = 1024
BATCH = 8
SEQ = 2048
DEPTH = 4

HEAD_DIM = 64
FOX_HEADS = 8
FOX_WIDTH = FOX_HEADS * HEAD_DIM
Q_BLOCK = 128
SWA_Q_HEADS = 8
SWA_KV_HEADS = 2
SWA_WIDTH = SWA_Q_HEADS * HEAD_DIM
SWA_KV_WIDTH = SWA_KV_HEADS * HEAD_DIM
WINDOW = 128
GLA_HEADS = 4
GLA_DK = D_MODEL // 2
GLA_DV = D_MODEL
GLA_HEAD_K = GLA_DK // GLA_HEADS
GLA_HEAD_V = GLA_DV // GLA_HEADS
GLA_RANK = 16
GLA_TAU = 16.0
GLA_CHUNK = 64
D_FF = 2816
N_BRANCH = 3
EPS = 1e-6
NEG_INF = -1e30
SPLIT_SIZES = (FOX_WIDTH, FOX_WIDTH, FOX_WIDTH, FOX_HEADS,
               SWA_WIDTH, SWA_KV_WIDTH, SWA_KV_WIDTH,
               GLA_DK, GLA_DK, GLA_DV, GLA_RANK, GLA_DV,
               N_BRANCH * D_MODEL)
D_IN = sum(SPLIT_SIZES)

kernel_name = "hybrid_fox_swa_gla_macaron"


def rmsnorm(x, w):
    xf = x.astype(jnp.float32)
    y = xf * lax.rsqrt(jnp.mean(xf * xf, axis=-1, keepdims=True) + EPS)
    return (y * w.astype(jnp.float32)).astype(x.dtype)


def swiglu(h, w_gu, w_down):
    g, u = jnp.split(h @ w_gu, 2, axis=-1)
    return (jax.nn.silu(g) * u) @ w_down


def forgetting_attention(q, k, v, f_logit, f_bias):
    B, S, H, Dh = q.shape
    nb = S // Q_BLOCK
    log_f = jax.nn.log_sigmoid(f_logit.astype(jnp.float32) + f_bias.astype(jnp.float32))
    c = jnp.cumsum(log_f, axis=1).transpose(0, 2, 1)
    qb = q.reshape(B, nb, Q_BLOCK, H, Dh).transpose(1, 0, 3, 2, 4)
    cq = c.reshape(B, H, nb, Q_BLOCK).transpose(2, 0, 1, 3)
    kpos = jnp.arange(S)
    scale = Dh ** -0.5

    def block(args):
        qi, ci, n = args
        s = jnp.einsum('bhqd,bkhd->bhqk', qi, k).astype(jnp.float32) * scale
        s = s + ci[..., None] - c[:, :, None, :]
        qpos = n * Q_BLOCK + jnp.arange(Q_BLOCK)
        s = jnp.where(kpos[None, :] <= qpos[:, None], s, NEG_INF)
        p = jax.nn.softmax(s, axis=-1)
        return jnp.einsum('bhqk,bkhd->bqhd', p.astype(v.dtype), v)

    o = lax.map(block, (qb, cq, jnp.arange(nb)))
    return o.transpose(1, 0, 2, 3, 4).reshape(B, S, H * Dh)


def sliding_window_attention(q, k, v, sinks):
    B, S, Hq, Dh = q.shape
    Hkv = k.shape[2]
    G = Hq // Hkv
    nb = S // WINDOW
    qb = q.reshape(B, nb, WINDOW, Hkv, G, Dh)
    kb = k.reshape(B, nb, WINDOW, Hkv, Dh)
    vb = v.reshape(B, nb, WINDOW, Hkv, Dh)
    prev = lambda t: jnp.concatenate([jnp.zeros_like(t[:, :1]), t[:, :-1]], axis=1)
    kw = jnp.concatenate([prev(kb), kb], axis=2)
    vw = jnp.concatenate([prev(vb), vb], axis=2)
    s = jnp.einsum('bnqhgd,bnkhd->bnhgqk', qb, kw).astype(jnp.float32) * (Dh ** -0.5)
    i = jnp.arange(WINDOW)[:, None]
    j = jnp.arange(2 * WINDOW)[None, :]
    band = (j > i) & (j <= i + WINDOW)
    has_prev = jnp.arange(nb)[:, None, None] > 0
    valid = band[None] & (has_prev | (j >= WINDOW)[None])
    s = jnp.where(valid[None, :, None, None], s, NEG_INF)
    sink = sinks.astype(jnp.float32).reshape(Hkv, G)[None, None, :, :, None, None]
    sink = jnp.broadcast_to(sink, s.shape[:-1] + (1,))
    p = jax.nn.softmax(jnp.concatenate([s, sink], axis=-1), axis=-1)[..., :-1]
    o = jnp.einsum('bnhgqk,bnkhd->bnqhgd', p.astype(v.dtype), vw)
    return o.reshape(B, S, Hq * Dh)


def gated_linear_attention(q, k, v, log_a):
    B, S, H, Dk = q.shape
    Dv = v.shape[-1]
    C = GLA_CHUNK
    N = S // C
    qf = q.astype(jnp.float32).reshape(B, N, C, H, Dk) * (Dk ** -0.5)
    kf = k.astype(jnp.float32).reshape(B, N, C, H, Dk)
    vf = v.astype(jnp.float32).reshape(B, N, C, H, Dv)
    b = jnp.cumsum(log_a.reshape(B, N, C, H, Dk), axis=2)
    b_last = b[:, :, -1:]
    q_t = qf * jnp.exp(b)
    k_t = kf * jnp.exp(-b)
    k_s = kf * jnp.exp(b_last - b)
    causal = jnp.tril(jnp.ones((C, C), dtype=bool))
    A = jnp.where(causal, jnp.einsum('bnihd,bnjhd->bnhij', q_t, k_t), 0.0)
    o_intra = jnp.einsum('bnhij,bnjhe->bnihe', A, vf)
    kv = jnp.einsum('bnjhd,bnjhe->bnhde', k_s, vf)
    decay = jnp.exp(b_last[:, :, 0])

    def step(state, inp):
        d, kv_n = inp
        return d[..., None] * state + kv_n, state

    state0 = jnp.zeros((B, H, Dk, Dv), jnp.float32)
    _, s_before = lax.scan(step, state0, (decay.transpose(1, 0, 2, 3), kv.transpose(1, 0, 2, 3, 4)))
    s_before = s_before.transpose(1, 0, 2, 3, 4)
    o_inter = jnp.einsum('bnihd,bnhde->bnihe', q_t, s_before)
    return (o_intra + o_inter).reshape(B, S, H, Dv)


def hybrid_mixer(h, w_in, fox_forget_bias, swa_sinks, gla_gate_w2, gla_gate_bias, gla_norm,
                 w_branch_fox, w_branch_swa, w_branch_gla, w_out):
    B, S, D = h.shape
    points = [int(p) for p in np.cumsum(SPLIT_SIZES)[:-1]]
    (fq, fk, fv, ff, sq, sk, sv, gq, gk, gv, glr, gr, gates) = jnp.split(h @ w_in, points, axis=-1)
    hd = lambda t, n: t.reshape(B, S, n, -1)
    o_fox = forgetting_attention(hd(fq, FOX_HEADS), hd(fk, FOX_HEADS), hd(fv, FOX_HEADS), ff, fox_forget_bias)
    o_swa = sliding_window_attention(hd(sq, SWA_Q_HEADS), hd(sk, SWA_KV_HEADS), hd(sv, SWA_KV_HEADS), swa_sinks)
    gate_logit = (glr @ gla_gate_w2 + gla_gate_bias).astype(jnp.float32)
    log_a = jax.nn.log_sigmoid(gate_logit) / GLA_TAU
    o = gated_linear_attention(hd(gq, GLA_HEADS), hd(gk, GLA_HEADS), hd(gv, GLA_HEADS), hd(log_a, GLA_HEADS))
    o = o * lax.rsqrt(jnp.mean(o * o, axis=-1, keepdims=True) + EPS) * gla_norm.astype(jnp.float32)
    o_gla = (o.reshape(B, S, GLA_DV) * jax.nn.silu(gr.astype(jnp.float32))).astype(h.dtype)
    g = jax.nn.sigmoid(gates.astype(jnp.float32)).astype(h.dtype).reshape(B, S, N_BRANCH, D)
    merged = (g[:, :, 0] * (o_fox @ w_branch_fox)
              + g[:, :, 1] * (o_swa @ w_branch_swa)
              + g[:, :, 2] * (o_gla @ w_branch_gla))
    return merged @ w_out


def setup_inputs(seed: int = 0) -> dict:
    key = jax.random.key(seed)
    ks = jax.random.split(key, 24)
    L, D, F = DEPTH, D_MODEL, D_FF
    f32 = jnp.float32
    nrm = lambda k, shape, fan_in: jax.random.normal(k, shape, f32) * (fan_in ** -0.5)
    gain = lambda k, shape: 1.0 + 0.05 * jax.random.normal(k, shape, f32)
    return {
        "x": jax.random.normal(ks[0], (BATCH, SEQ, D), f32),
        "ffn1_norm": gain(ks[1], (L, D)),
        "ffn1_w_gu": nrm(ks[2], (L, D, 2 * F), D),
        "ffn1_w_down": nrm(ks[3], (L, F, D), F),
        "mix_norm": gain(ks[4], (L, D)),
        "w_in": nrm(ks[5], (L, D, D_IN), D),
        "fox_forget_bias": jax.random.uniform(ks[6], (L, FOX_HEADS), f32, 1.0, 4.0),
        "swa_sinks": 0.5 * jax.random.normal(ks[7], (L, SWA_Q_HEADS), f32),
        "gla_gate_w2": nrm(ks[8], (L, GLA_RANK, GLA_DK), GLA_RANK),
        "gla_gate_bias": 0.1 * jax.random.normal(ks[9], (L, GLA_DK), f32),
        "gla_norm": gain(ks[10], (L, GLA_HEAD_V)),
        "w_branch_fox": nrm(ks[11], (L, FOX_WIDTH, D), FOX_WIDTH),
        "w_branch_swa": nrm(ks[12], (L, SWA_WIDTH, D), SWA_WIDTH),
        "w_branch_gla": nrm(ks[13], (L, GLA_DV, D), GLA_DV),
        "w_out": nrm(ks[14], (L, D, D), D),
        "ffn2_norm": gain(ks[15], (L, D)),
        "ffn2_w_gu": nrm(ks[16], (L, D, 2 * F), D),
        "ffn2_w_down": nrm(ks[17], (L, F, D), F),
        "final_norm": gain(ks[18], (D,)),
    }


def reference(x, ffn1_norm, ffn1_w_gu, ffn1_w_down, mix_norm, w_in, fox_forget_bias, swa_sinks,
              gla_gate_w2, gla_gate_bias, gla_norm, w_branch_fox, w_branch_swa, w_branch_gla, w_out,
              ffn2_norm, ffn2_w_gu, ffn2_w_down, final_norm):
    for l in range(DEPTH):
        x = x + 0.5 * swiglu(rmsnorm(x, ffn1_norm[l]), ffn1_w_gu[l], ffn1_w_down[l])
        x = x + hybrid_mixer(rmsnorm(x, mix_norm[l]), w_in[l], fox_forget_bias[l], swa_sinks[l],
                             gla_gate_w2[l], gla_gate_bias[l], gla_norm[l],
                             w_branch_fox[l], w_branch_swa[l], w_branch_gla[l], w_out[l])
        x = x + 0.5 * swiglu(rmsnorm(x, ffn2_norm[l]), ffn2_w_gu[l], ffn2_w_down[l])
    return rmsnorm(x, final_norm)
```

```python
import contextlib
import numpy as np
import concourse.bass as bass
import concourse.mybir as mybir
from concourse.bass_utils import run_bass_kernel_spmd

F32, BF16 = mybir.dt.float32, mybir.dt.bfloat16
AF = mybir.ActivationFunctionType
ALU = mybir.AluOpType

D = 1024
KD = 8
DFF = 2816
NF = 22
DIN = 8472
EPS = 1e-6
C_FQ, C_FK, C_FV, C_FF, C_SQ, C_SK, C_SV = 0, 512, 1024, 1536, 1544, 2056, 2184
C_GQ, C_GK, C_GV, C_GLR, C_GR, C_GATE = 2312, 2824, 3336, 4360, 4376, 5400
NEG = -30000.0

ENGS = ("pe", "act", "dve", "pool", "sp")
N_DMA_SEMS = 12


class Op:
    __slots__ = ("eng", "fn", "deps", "odeps", "gid", "dma", "signals", "sem", "val", "waits", "snap",
                 "cost", "nbytes", "t_end", "sched", "barrier")

    def __init__(self, eng, fn, gid, dma, cost=300.0, nbytes=0):
        self.eng, self.fn, self.gid, self.dma = eng, fn, gid, dma
        self.deps = set()
        self.odeps = set()
        self.signals = False
        self.sem = None
        self.val = 0
        self.waits = []
        self.snap = None
        self.cost = cost
        self.nbytes = nbytes
        self.t_end = 0.0
        self.sched = False
        self.barrier = False


SEM_LAT = 150.0
import os as _os
WINDOW = {"pe": 64, "act": 32, "dve": 32, "pool": 1, "sp": 1}
if _os.environ.get("MK_WINDOW"):
    WINDOW = dict(zip(("pe", "act", "dve", "pool", "sp"), map(int, _os.environ["MK_WINDOW"].split(","))))


class Sched:
    def __init__(self):
        self.ops = []
        self.last_w = {}
        self.readers = {}
        self.dma_rr = {e: 0 for e in ENGS}
        self.dma_last = {}
        self.arena_dmas = {e: [] for e in ENGS}
        self.last_compute = {e: None for e in ENGS}

    def op(self, eng, fn, reads=(), writes=(), dma=False, arena=True, cost=300.0, nbytes=0):
        o = Op(eng, fn, len(self.ops), dma, cost, nbytes)
        deps = o.deps
        for k in reads:
            w = self.last_w.get(k)
            if w is not None:
                deps.add(w)
        for k in writes:
            w = self.last_w.get(k)
            if w is not None:
                deps.add(w)
            for r in self.readers.get(k, ()):
                deps.add(r)
        for k in reads:
            self.readers.setdefault(k, []).append(o)
        for k in writes:
            self.last_w[k] = o
            self.readers[k] = []
        if dma:
            slot = self.dma_rr[eng]
            self.dma_rr[eng] = (slot + 1) % N_DMA_SEMS
            prev = self.dma_last.get((eng, slot))
            if prev is not None:
                deps.add(prev)
            self.dma_last[(eng, slot)] = o
            o.sem = ("dma", eng, slot)
            o.val = (prev.val if prev is not None else 0) + 16
            if arena:
                self.arena_dmas[eng].append(o)
        else:
            self.last_compute[eng] = o
        if eng == "pe":
            pp = {d for d in deps if d.eng == "pe" and not d.dma}
            o.odeps = pp
            o.deps = deps - pp
        self.ops.append(o)
        return o

    def barrier(self):
        alldeps = set()
        for e in ENGS:
            for o in self.arena_dmas[e]:
                alldeps.add(o)
            self.arena_dmas[e] = []
        for e in ENGS:
            o = Op(e, None, len(self.ops), False, 0.0)
            o.deps = set(alldeps)
            o.barrier = True
            self.ops.append(o)

    def _schedule(self):
        order = []
        t_free = {e: 0.0 for e in ENGS}
        dma_free = [0.0]
        seg = []

        def flush(seg):
            pend = {e: [o for o in seg if o.eng == e] for e in ENGS}
            head = {e: 0 for e in ENGS}
            left = len(seg)
            while left:
                best = None
                bkey = None
                for e in ENGS:
                    q = pend[e]
                    i = head[e]
                    seen = 0
                    w = WINDOW[e]
                    while i < len(q) and seen < w:
                        o = q[i]
                        i += 1
                        if o.sched:
                            continue
                        seen += 1
                        ok = True
                        ready = 0.0
                        for d in o.deps:
                            if not d.sched:
                                ok = False
                                break
                            if d.t_end + SEM_LAT > ready:
                                ready = d.t_end + SEM_LAT
                        if not ok:
                            continue
                        for d in o.odeps:
                            if not d.sched:
                                ok = False
                                break
                        if not ok:
                            continue
                        start = max(ready, t_free[e])
                        key = (start, o.gid)
                        if bkey is None or key < bkey:
                            best, bkey = o, key
                o = best
                assert o is not None, "scheduler stuck"
                e = o.eng
                start = bkey[0]
                if o.dma:
                    issue = 1000.0 if e == "pool" else 600.0
                    t_free[e] = start + issue
                    ts = max(start + 1500.0, dma_free[0])
                    o.t_end = ts + o.nbytes / 280.0
                    dma_free[0] = o.t_end
                else:
                    o.t_end = start + o.cost
                    t_free[e] = o.t_end
                o.sched = True
                order.append(o)
                left -= 1
                q = pend[e]
                while head[e] < len(q) and q[head[e]].sched:
                    head[e] += 1

        i = 0
        n = len(self.ops)
        while i < n:
            o = self.ops[i]
            if o.barrier:
                flush(seg)
                seg = []
                tb = max(t_free.values())
                lastc = set()
                for e in ENGS:
                    for oo in reversed(order):
                        if oo.eng == e and not oo.dma and oo.fn is not None:
                            lastc.add(oo)
                            break
                while i < n and self.ops[i].barrier:
                    b = self.ops[i]
                    b.deps |= lastc
                    b.sched = True
                    b.t_end = tb
                    order.append(b)
                    i += 1
                for e in ENGS:
                    t_free[e] = tb
                continue
            seg.append(o)
            i += 1
        flush(seg)
        self.order = order
        self.by_eng = {e: [o for o in order if o.eng == e] for e in ENGS}
        self.t_model = max(t_free.values())

    def finalize(self):
        self._schedule()
        for o in self.order:
            for d in o.deps:
                d.signals = True
        cnt = {e: 0 for e in ENGS}
        for o in self.order:
            if not o.dma and o.signals:
                cnt[o.eng] += 1
                o.sem = ("eng", o.eng)
                o.val = cnt[o.eng]
        known = {e: {} for e in ENGS}
        for o in self.order:
            kn = known[o.eng]
            need = {}
            for d in o.deps:
                if kn.get(d.sem, 0) >= d.val:
                    continue
                if need.get(d.sem, (0, None))[0] < d.val:
                    need[d.sem] = (d.val, d)
            for sem, (val, d) in sorted(need.items(), key=lambda kv: -kv[1][1].gid):
                if kn.get(sem, 0) >= val:
                    continue
                o.waits.append((sem, val))
                if d.snap:
                    for s_, v in d.snap.items():
                        if kn.get(s_, 0) < v:
                            kn[s_] = v
                if kn.get(sem, 0) < val:
                    kn[sem] = val
            if o.dma:
                o.snap = dict(kn)
                o.snap[o.sem] = o.val
            elif o.signals:
                o.snap = dict(kn)
                o.snap[o.sem] = o.val

    def sem_keys(self):
        keys = [("eng", e) for e in ENGS]
        for e in ENGS:
            for s_ in range(N_DMA_SEMS):
                if (e, s_) in self.dma_last:
                    keys.append(("dma", e, s_))
        return keys

    def emit(self, block, sems):
        def run(eng_name):
            def body(eng):
                for o in self.by_eng[eng_name]:
                    for sem, val in o.waits:
                        eng.wait_ge(sems[sem], val)
                    if o.fn is None:
                        continue
                    ins = o.fn(eng)
                    if o.dma:
                        ins.then_inc(sems[o.sem], 16)
                    elif o.signals:
                        ins.then_inc(sems[o.sem], 1)
            return body
        block.tensor(run("pe"))
        block.scalar(run("act"))
        block.vector(run("dve"))
        block.gpsimd(run("pool"))
        block.sync(run("sp"))


def _ktile(w2d, c0, c1):
    K = w2d.shape[0]
    kc = K // 128
    return np.ascontiguousarray(
        w2d[:, c0:c1].reshape(kc, 128, c1 - c0).transpose(1, 0, 2)).reshape(128, kc * (c1 - c0))


def weight_plan(depth):
    plan = []
    for l in range(depth):
        for which in (1, 2):
            gu, dn = f"ffn{which}_w_gu", f"ffn{which}_w_down"
            for f in range(NF):
                def b(inp, l=l, f=f, gu=gu):
                    w = inp[gu][l]
                    g = w[:, f * 128:(f + 1) * 128].reshape(8, 128, 128)
                    u = w[:, DFF + f * 128:DFF + (f + 1) * 128].reshape(8, 128, 128)
                    t = np.concatenate([g, u], axis=2)
                    return np.ascontiguousarray(t.transpose(1, 0, 2)).reshape(128, 2048)
                plan.append(((l, which, "gu", f), b, 2048))
            for j in range(NF // 2):
                def b(inp, l=l, j=j, dn=dn):
                    w = inp[dn][l][j * 256:(j + 1) * 256, :]
                    return _ktile(w, 0, 1024)
                plan.append(((l, which, "dn", j), b, 2048))

        def wi(name, c0, c1, l=l):
            def b(inp, c0=c0, c1=c1, l=l):
                return _ktile(inp["w_in"][l], c0, c1)
            plan.append(((l, name), b, 8 * (c1 - c0)))
        for j in range(4):
            wi(("fq", j), C_FQ + j * 128, C_FQ + (j + 1) * 128)
            wi(("fk", j), C_FK + j * 128, C_FK + (j + 1) * 128)
            wi(("sq", j), C_SQ + j * 128, C_SQ + (j + 1) * 128)
            wi(("gq", j), C_GQ + j * 128, C_GQ + (j + 1) * 128)
            wi(("gk", j), C_GK + j * 128, C_GK + (j + 1) * 128)
            wi(("gv", j), C_GV + j * 256, C_GV + (j + 1) * 256)
            wi(("gr", j), C_GR + j * 256, C_GR + (j + 1) * 256)
        for j in range(2):
            wi(("fv", j), C_FV + j * 256, C_FV + (j + 1) * 256)
        wi(("ff", 0), C_FF, C_FF + 8)
        wi(("sk", 0), C_SK, C_SK + 128)
        wi(("sv", 0), C_SV, C_SV + 128)
        wi(("glr", 0), C_GLR, C_GLR + 16)
        for i in range(3):
            for dc in range(8):
                wi(("gate", i, dc), C_GATE + i * 1024 + dc * 128, C_GATE + i * 1024 + (dc + 1) * 128)
        for i, nm in enumerate(("w_branch_fox", "w_branch_swa", "w_branch_gla")):
            kc = 8 if i == 2 else 4
            for dc in range(8):
                def b(inp, l=l, nm=nm, dc=dc):
                    return _ktile(inp[nm][l], dc * 128, (dc + 1) * 128)
                plan.append(((l, ("br", i, dc)), b, kc * 128))
        for j in range(4):
            def b(inp, l=l, j=j):
                return _ktile(inp["w_out"][l][j * 256:(j + 1) * 256, :], 0, 1024)
            plan.append(((l, ("wo", j)), b, 2048))

        def b(inp, l=l):
            t = np.zeros((128, 512), np.float32)
            t[0:16, :] = inp["gla_gate_w2"][l]
            return t
        plan.append(((l, ("w2", 0)), b, 512))
    return plan


def build_program(S_len, depth):
    T = S_len // 128
    NB = S_len // 512
    HALF = min(1024, S_len)
    NH = S_len // HALF
    HB = HALF // 512
    HT = HALF // 128

    nc = bass.Bass("TRN2", target_bir_lowering=False)
    plan = weight_plan(depth)
    woff = {}
    tot = 0
    for key, _, n in plan:
        woff[key] = (tot, n)
        tot += n
    x_d = nc.dram_tensor("x", [S_len, D], F32, kind="ExternalInput").ap()
    W_d = nc.dram_tensor("W", [128, tot], F32, kind="ExternalInput").ap()
    nrm_d = nc.dram_tensor("nrm", [3 * depth + 1, D], F32, kind="ExternalInput").ap()
    fb_d = nc.dram_tensor("fbias", [depth, 8, 1], F32, kind="ExternalInput").ap()
    sk_d = nc.dram_tensor("sinks", [depth, 8], F32, kind="ExternalInput").ap()
    gb_d = nc.dram_tensor("gbias", [depth, 128, 4], F32, kind="ExternalInput").ap()
    gn_d = nc.dram_tensor("gnorm", [depth, 256], F32, kind="ExternalInput").ap()
    y_d = nc.dram_tensor("y", [S_len, D], F32, kind="ExternalOutput").ap()
    cscr = [nc.dram_tensor(f"cscr{l}", [8, 6, S_len], BF16).ap() for l in range(depth)]

    S = Sched()
    uid = [0]

    BASE = 16512
    LIMIT = 229376
    cur = [BASE]

    def alloc(name, shape, dt, at=None):
        uid[0] += 1
        nb = int(np.prod(shape[1:])) * (4 if dt == F32 else 2)
        nb = (nb + 63) // 64 * 64
        if at is None:
            off = cur[0]
            cur[0] += nb
        else:
            off = at
        assert off + nb <= LIMIT, (name, off, nb)
        return nc.alloc_sbuf_tensor_at(f"{name}_{uid[0]}", list(shape), dt, offset=off), off + nb

    X, _ = alloc("X", [128, T, D], F32)
    ident, _ = alloc("ident", [128, 128], BF16)
    negA, _ = alloc("negA", [128, 4, 128], BF16)
    negB, _ = alloc("negB", [128, 4, 128], BF16)
    m01, _ = alloc("m01", [128, 128], BF16)
    cmask, _ = alloc("cmask", [128, 512], F32)
    onesf, _ = alloc("onesf", [128, 512], F32)
    epsT, _ = alloc("epsT", [128, 1], F32)
    wb, _ = alloc("wb", [128, D], F32)
    ssum, _ = alloc("ssum", [128, 16], F32)
    rstd, _ = alloc("rstd", [128, 16], F32)
    NRING = 5
    ring = [alloc(f"ring{i}", [128, 2048], BF16)[0] for i in range(NRING)]
    ARENA = cur[0]
    ring_i = [0]

    PS = [nc.alloc_psum_tensor(f"ps{i}", [128, 512], F32) for i in range(8)]
    pools = {"mm": [0, 1, 2, 3], "acc": [4, 5], "aux": [6, 7]}
    prr = {k: 0 for k in pools}

    def set_pools(kind):
        if kind == "wide":
            pools.update({"mm": [0, 1, 2, 3, 4, 5], "acc": [4, 5], "aux": [6, 7]})
        else:
            pools.update({"mm": [0, 1, 2, 3], "acc": [4, 5], "aux": [6, 7]})

    def bank(pool):
        b = pools[pool][prr[pool] % len(pools[pool])]
        prr[pool] += 1
        return b

    def psk(b):
        return ("ps", b)

    preloaded = {}

    def phase(prefetch=()):
        for k in prefetch:
            if k not in preloaded:
                preloaded[k] = load_w(k)
        S.barrier()

    def load_w(key, dst=None, dkey=None, rows=128):
        off, n = woff[key]
        if dst is None and key in preloaded:
            return preloaded.pop(key)
        if dst is None:
            slot = ring_i[0] % NRING
            ring_i[0] += 1
            t = ring[slot]
            S.op("pool", lambda e, t=t, off=off, n=n: e.dma_start(out=t[:, 0:n], in_=W_d[:, off:off + n]),
                 writes=[("ring", slot)], dma=True, arena=False, nbytes=n * 512)
            return t, ("ring", slot)
        if len(dst.shape) == 3:
            dst = dst.rearrange("p a b -> p (a b)")
        S.op("pool", lambda e, off=off, n=n, dst=dst: e.dma_start(out=dst, in_=W_d[0:rows, off:off + n]),
             writes=[dkey], dma=True, arena=True, nbytes=n * 4 * rows)
        return None

    def mm(b, pairs, reads, rows=128, cols=(0, 512), outs=None):
        def fn(e):
            n = len(pairs)
            ins = None
            for i, p in enumerate(pairs):
                c0, c1 = p[2] if len(p) > 2 else cols
                ins = e.matmul(PS[b][0:rows, c0:c1], lhsT=p[0], rhs=p[1], start=(i == 0), stop=(i == n - 1))
            return ins
        cst = sum(max(((p[2] if len(p) > 2 else cols)[1] - (p[2] if len(p) > 2 else cols)[0]), 64) / 2.4 + 12.0
                  for p in pairs)
        S.op("pe", fn, reads=reads, writes=[psk(b)], cost=cst)

    def act(out, in_, func, reads, writes, bias=None, scale=None, accum=None):
        kw = {}
        if bias is not None:
            kw["bias"] = bias
        if scale is not None:
            kw["scale"] = scale
        if accum is not None:
            kw["accum_out"] = accum
        nfree = int(np.prod(in_.shape[1:]))
        S.op("act", lambda e: e.activation(out=out, in_=in_, func=func, **kw), reads=reads, writes=writes,
             cost=220.0 + nfree / 1.2 + (100.0 if accum is not None else 0.0))

    def dve(fn, reads, writes, n=512):
        S.op("dve", fn, reads=reads, writes=writes, cost=120.0 + n / 0.96)

    def sp_dma(out, in_, reads=(), writes=(), arena=True):
        nb = int(np.prod(out.shape)) * 4
        return S.op("sp", lambda e: e.dma_start(out=out, in_=in_), reads=reads, writes=writes, dma=True, arena=arena,
                    nbytes=nb)

    S.op("pool", lambda e: e.memset(ident[:, :], 1.0), writes=["ident"])
    S.op("pool", lambda e: e.affine_select(out=ident[:, :], in_=ident[:, :], pattern=[[-1, 128]],
                                           compare_op=ALU.is_equal, fill=0.0, base=0, channel_multiplier=1),
         reads=["ident"], writes=["ident"])
    S.op("pool", lambda e: e.memset(negA[:, :, :], NEG), writes=["negA"])
    S.op("pool", lambda e: e.affine_select(out=negA[:, :, :], in_=negA[:, :, :], pattern=[[0, 4], [-1, 128]],
                                           compare_op=ALU.is_gt, fill=0.0, base=0, channel_multiplier=1),
         reads=["negA"], writes=["negA"])
    S.op("pool", lambda e: e.memset(negB[:, :, :], NEG), writes=["negB"])
    S.op("pool", lambda e: e.affine_select(out=negB[:, :, :], in_=negB[:, :, :], pattern=[[0, 4], [1, 128]],
                                           compare_op=ALU.is_ge, fill=0.0, base=0, channel_multiplier=-1),
         reads=["negB"], writes=["negB"])
    S.op("pool", lambda e: e.memset(m01[:, :], 1.0), writes=["m01"])
    S.op("pool", lambda e: e.affine_select(out=m01[:, :], in_=m01[:, :], pattern=[[1, 128]],
                                           compare_op=ALU.is_ge, fill=0.0, base=0, channel_multiplier=-1),
         reads=["m01"], writes=["m01"])
    S.op("pool", lambda e: e.memset(cmask[:, :], 1.0), writes=["cmask"])
    S.op("pool", lambda e: e.memset(cmask[:, :].rearrange("p (c j) -> p c j", j=128)[:, :, 0:1], 0.0),
         reads=["cmask"], writes=["cmask"])
    S.op("pool", lambda e: e.memset(onesf[:, :], 1.0), writes=["onesf"])
    S.op("pool", lambda e: e.memset(epsT[:, :], EPS), writes=["epsT"])
    consts = ["ident", "negA", "negB", "m01", "cmask", "onesf", "epsT"]

    for t in range(T):
        sp_dma(X[:, t, :], x_d[t * 128:(t + 1) * 128, :], writes=[("x", t, 0), ("x", t, 1)])

    def xk(t):
        return [("x", t, 0), ("x", t, 1)]

    def norm_to_hT(nrow, tiles, hT, col0, scratch):
        junk, hbs = scratch
        sp_dma(wb[:, :], nrm_d[nrow:nrow + 1, :].broadcast_to([128, D]), writes=["wb"])
        nt = len(tiles)
        for i, t in enumerate(tiles):
            act(junk[:, :], X[:, t, :], AF.Square, reads=xk(t), writes=["junk", ("ss", i)],
                accum=ssum[:, i:i + 1])
        act(rstd[:, 0:nt], ssum[:, 0:nt], AF.Sqrt, reads=[("ss", i) for i in range(nt)] + ["epsT"],
            writes=["rstd"], bias=epsT[:, 0:1], scale=1.0 / D)
        dve(lambda e: e.reciprocal(out=rstd[:, 0:nt], in_=rstd[:, 0:nt]), reads=["rstd"], writes=["rstd"])
        for i, t in enumerate(tiles):
            hb = hbs[i % 2]
            hk = ("hb", i % 2)
            dve(lambda e, hb=hb, t=t, i=i: e.scalar_tensor_tensor(
                out=hb[:, :], in0=X[:, t, :], scalar=rstd[:, i:i + 1], in1=wb[:, :],
                op0=ALU.mult, op1=ALU.mult), reads=xk(t) + ["rstd", "wb"], writes=[hk], n=1536)
            b = bank("aux")
            pst = PS[b].bitcast(BF16)

            def tr(e, hb=hb, pst=pst):
                ins = None
                for k in range(KD):
                    ins = e.transpose(out=pst[:, k * 128:(k + 1) * 128], in_=hb[:, k * 128:(k + 1) * 128],
                                      identity=ident[:, :])
                return ins
            S.op("pe", tr, reads=[hk, "ident"], writes=[psk(b)], cost=560.0)
            c = col0 + i * 128
            act(hT[:, :, c:c + 128], pst[:, :].rearrange("p (k j) -> p k j", k=KD), AF.Copy,
                reads=[psk(b)], writes=[("hT", c // 128)])

    def ffn(l, which):
        phase([(l, which, "gu", f) for f in range(3)])
        set_pools("wide")
        a = ARENA
        hTh, a = alloc("hTh", [128, KD, HALF], BF16, a)
        aT, a = alloc("aT", [128, NF, HALF], BF16, a)
        wdn, a = alloc("wdn", [128, NF, D], BF16, a)
        junk, a = alloc("junk", [128, D], BF16, a)
        hb0, a = alloc("hb0", [128, D], BF16, a)
        hb1, a = alloc("hb1", [128, D], BF16, a)
        sgs = []
        for i in range(3):
            t_, a = alloc("sg", [128, 512], BF16, a)
            sgs.append(t_)
        nrow = 3 * l + (0 if which == 1 else 2)
        sgi = 0
        for h in range(NH):
            tiles = list(range(h * HT, (h + 1) * HT))
            norm_to_hT(nrow, tiles, hTh, 0, (junk, (hb0, hb1)))
            for f in range(NF):
                wt, wk = load_w((l, which, "gu", f))
                if h == 0 and f % 2 == 1:
                    j = f // 2
                    load_w((l, which, "dn", j), dst=wdn[:, 2 * j:2 * j + 2, :], dkey=("wdn", j))
                wv = wt[:, :].rearrange("p (k c) -> p k c", k=KD)
                for blk in range(HB):
                    hk = [("hT", blk * 4 + i) for i in range(4)]
                    bg, bu = bank("mm"), bank("mm")
                    mm(bg, [(wv[:, k, 0:128], hTh[:, k, blk * 512:(blk + 1) * 512]) for k in range(KD)],
                       reads=[wk] + hk)
                    mm(bu, [(wv[:, k, 128:256], hTh[:, k, blk * 512:(blk + 1) * 512]) for k in range(KD)],
                       reads=[wk] + hk)
                    sg = sgs[sgi % 3]
                    sgk = ("sg", sgi % 3)
                    sgi += 1
                    act(sg[:, :], PS[bg][:, :], AF.Silu, reads=[psk(bg)], writes=[sgk])
                    dve(lambda e, sg=sg, bu=bu, f=f, blk=blk: e.tensor_tensor(
                        out=aT[:, f, blk * 512:(blk + 1) * 512], in0=PS[bu][:, :], in1=sg[:, :], op=ALU.mult),
                        reads=[psk(bu), sgk], writes=[("aT", f, blk)])
            for i, t in enumerate(tiles):
                for dh in range(2):
                    b = bank("mm")
                    mm(b, [(aT[:, f, i * 128:(i + 1) * 128], wdn[:, f, dh * 512:(dh + 1) * 512]) for f in range(NF)],
                       reads=[("aT", f, i // 4) for f in range(NF)] + [("wdn", j) for j in range(NF // 2)])
                    dve(lambda e, b=b, t=t, dh=dh: e.scalar_tensor_tensor(
                        out=X[:, t, dh * 512:(dh + 1) * 512], in0=PS[b][:, :], scalar=0.5,
                        in1=X[:, t, dh * 512:(dh + 1) * 512], op0=ALU.mult, op1=ALU.add),
                        reads=[psk(b), ("x", t, dh)], writes=[("x", t, dh)])

    def merge_block(l, br, oT, okeys, kc, mergedT, blk, tmpbufs, first):
        c0, c1 = blk * 512, (blk + 1) * 512
        hk = [("hT", blk * 4 + i) for i in range(4)]
        sgt, tmp = tmpbufs
        for dc in range(KD):
            wbt, wbk = load_w((l, ("br", br, dc)))
            wgt, wgk = load_w((l, ("gate", br, dc)))
            wbv = wbt[:, 0:kc * 128].rearrange("p (k c) -> p k c", k=kc)
            wgv = wgt[:, 0:1024].rearrange("p (k c) -> p k c", k=KD)
            bb, bg = bank("mm"), bank("mm")
            mm(bb, [(wbv[:, k, :], oT(k)) for k in range(kc)], reads=[wbk] + okeys)
            mm(bg, [(wgv[:, k, :], hT_cur[0][:, k, c0:c1]) for k in range(KD)], reads=[wgk] + hk)
            i = dc % 2
            act(sgt[i][:, :], PS[bg][:, :], AF.Sigmoid, reads=[psk(bg)], writes=[("sgt", i)])
            if first:
                dve(lambda e, bb=bb, i=i, dc=dc: e.tensor_tensor(
                    out=mergedT[:, dc, c0:c1], in0=PS[bb][:, :], in1=sgt[i][:, :], op=ALU.mult),
                    reads=[psk(bb), ("sgt", i)], writes=[("mg", dc, blk)])
            else:
                dve(lambda e, bb=bb, i=i: e.tensor_tensor(
                    out=tmp[i][:, :], in0=PS[bb][:, :], in1=sgt[i][:, :], op=ALU.mult),
                    reads=[psk(bb), ("sgt", i)], writes=[("mtmp", id(tmp[i]))])
                dve(lambda e, i=i, dc=dc: e.tensor_tensor(
                    out=mergedT[:, dc, c0:c1], in0=mergedT[:, dc, c0:c1], in1=tmp[i][:, :], op=ALU.add),
                    reads=[("mtmp", id(tmp[i])), ("mg", dc, blk)], writes=[("mg", dc, blk)])

    hT_cur = [None]

    def mixer(l):
        phase([(l, ("ff", 0)), (l, ("fv", 0)), (l, ("fv", 1))])
        set_pools("narrow")
        a = ARENA
        hT, a = alloc("hT", [128, KD, S_len], BF16, a)
        hT_cur[0] = hT
        MREG = a
        mergedT, a = alloc("mergedT", [128, KD, S_len], BF16, a)
        LOC = a
        allhT = [("hT", t) for t in range(T)]

        Vf, _ = alloc("Vf", [128, T, 8, 128], BF16, MREG)
        a = LOC
        o_foxT, a = alloc("ofoxT", [128, 4, S_len], BF16, a)
        LOCA = a
        junk, a = alloc("junk", [128, D], BF16, a)
        hb0, a = alloc("hb0", [128, D], BF16, a)
        hb1, a = alloc("hb1", [128, D], BF16, a)
        fbn, a = alloc("fbn", [8, 1], F32, a)
        e1, a = alloc("e1", [8, 512], F32, a)
        nlf, a = alloc("nlf", [8, 512], F32, a)
        cnb = []
        for i in range(2):
            t_, a = alloc("cn", [8, 512], F32, a)
            cnb.append(t_)
        r1, a = alloc("r1", [8, 512], F32, a)
        pk, a = alloc("pk", [8, 3, 512], BF16, a)
        pq, a = alloc("pq", [8, 3, 512], BF16, a)

        norm_to_hT(3 * l + 1, list(range(T)), hT, 0, (junk, (hb0, hb1)))
        dve(lambda e: e.memset(Vf[:, :, :, 64:128], 1.0), reads=[], writes=["Vf1"])
        sp_dma(fbn[:, :], fb_d[l, :, :], writes=["fbn"])
        act(fbn[:, :], fbn[:, :], AF.Copy, reads=["fbn"], writes=["fbn"], scale=-1.0)
        wt, wk = load_w((l, ("ff", 0)))
        wv = wt[:, 0:64].rearrange("p (k c) -> p k c", k=KD)
        for blk in range(NB):
            c0, c1 = blk * 512, (blk + 1) * 512
            hk = [("hT", blk * 4 + i) for i in range(4)]
            b = bank("mm")
            mm(b, [(wv[:, k, :], hT[:, k, c0:c1]) for k in range(KD)], reads=[wk] + hk, rows=8)
            act(e1[:, :], PS[b][0:8, :], AF.Exp, reads=[psk(b), "fbn"], writes=["e1"], bias=fbn[:, 0:1], scale=-1.0)
            act(nlf[:, :], e1[:, :], AF.Ln, reads=["e1"], writes=["nlf"], bias=1.0)
            cn, cp = cnb[blk % 2], cnb[(blk + 1) % 2]
            if blk == 0:
                dve(lambda e, cn=cn: e.tensor_tensor_scan(out=cn[:, :], data0=onesf[0:8, :], data1=nlf[:, :],
                                                          initial=0.0, op0=ALU.mult, op1=ALU.add),
                    reads=["nlf", "onesf"], writes=[("cn", blk % 2)])
            else:
                dve(lambda e, cn=cn, cp=cp: e.tensor_tensor_scan(out=cn[:, :], data0=onesf[0:8, :], data1=nlf[:, :],
                                                                 initial=cp[:, 511:512], op0=ALU.mult, op1=ALU.add),
                    reads=["nlf", "onesf", ("cn", (blk + 1) % 2)], writes=[("cn", blk % 2)])
            ck = ("cn", blk % 2)
            dve(lambda e, cn=cn: e.tensor_copy(out=pk[:, 0, :], in_=cn[:, :]), reads=[ck], writes=["pk0"])
            dve(lambda e, cn=cn: e.tensor_tensor(out=r1[:, :], in0=cn[:, :], in1=pk[:, 0, :], op=ALU.subtract),
                reads=[ck, "pk0"], writes=["r1"])
            dve(lambda e: e.tensor_copy(out=pk[:, 1, :], in_=r1[:, :]), reads=["r1"], writes=["pk1"])
            dve(lambda e: e.tensor_tensor(out=r1[:, :], in0=r1[:, :], in1=pk[:, 1, :], op=ALU.subtract),
                reads=["r1", "pk1"], writes=["r1"])
            dve(lambda e: e.tensor_copy(out=pk[:, 2, :], in_=r1[:, :]), reads=["r1"], writes=["pk2"])
            act(pq[:, :, :], pk[:, :, :], AF.Copy, reads=["pk0", "pk1", "pk2"], writes=["pq"], scale=-1.0)
            sp_dma(cscr[l][:, 0:3, c0:c1], pk[:, :, :], reads=["pk0", "pk1", "pk2"], writes=[("cscr", l)])
            sp_dma(cscr[l][:, 3:6, c0:c1], pq[:, :, :], reads=["pq"], writes=[("cscr", l)])
        wts = [load_w((l, ("fv", j))) for j in range(2)]
        for t in range(T):
            for j in range(2):
                wt, wk = wts[j]
                wv = wt[:, :].rearrange("p (k c) -> p k c", k=KD)
                b = bank("mm")
                mm(b, [(hT[:, k, t * 128:(t + 1) * 128], wv[:, k, :]) for k in range(KD)],
                   reads=[wk, ("hT", t)], cols=(0, 256))
                act(Vf[:, t, 4 * j:4 * j + 4, 0:64], PS[b][:, 0:256].rearrange("p (h d) -> p h d", h=4), AF.Copy,
                    reads=[psk(b)], writes=[("Vf", t, j)])
        phase([(l, ("fq", 0)), (l, ("fk", 0))])
        a = LOCA
        qk = []
        for i in range(4):
            t_, a = alloc("qk", [70, S_len], BF16, a)
            qk.append(t_)
        pbuf = []
        for i in range(3):
            t_, a = alloc("pT", [128, 512], BF16, a)
            pbuf.append(t_)
        rden, a = alloc("rden", [64, 512], F32, a)
        pi = 0
        for j in range(4):
            qA, qB, kA, kB = qk
            wq, wqk = load_w((l, ("fq", j)))
            wkk, wkkk = load_w((l, ("fk", j)))
            wqv = wq[:, 0:1024].rearrange("p (k c) -> p k c", k=KD)
            wkv = wkk[:, 0:1024].rearrange("p (k c) -> p k c", k=KD)
            for nm, tl in (("qA", qA), ("qB", qB), ("kA", kA), ("kB", kB)):
                dve(lambda e, tl=tl: e.memset(tl[64:70, :], 1.0), reads=[], writes=[(nm, "aug")])
            for r, (qt_, kt_, qn, kn) in enumerate(((qA, kA, "qA", "kA"), (qB, kB, "qB", "kB"))):
                h = 2 * j + r
                sp_dma(kt_[64:67, :], cscr[l][h, 0:3, :], reads=[("cscr", l)], writes=[(kn, "aug")])
                sp_dma(qt_[67:70, :], cscr[l][h, 3:6, :], reads=[("cscr", l)], writes=[(qn, "aug")])
            for blk in range(NB):
                c0, c1 = blk * 512, (blk + 1) * 512
                hk = [("hT", blk * 4 + i) for i in range(4)]
                b = bank("mm")
                mm(b, [(wqv[:, k, :], hT[:, k, c0:c1]) for k in range(KD)], reads=[wqk] + hk)
                act(qA[0:64, c0:c1], PS[b][0:64, :], AF.Copy, reads=[psk(b)], writes=[("qA", blk)], scale=0.125)
                dve(lambda e, b=b, c0=c0, c1=c1: e.tensor_scalar(
                    out=qB[0:64, c0:c1], in0=PS[b][64:128, :], scalar1=0.125, scalar2=None, op0=ALU.mult),
                    reads=[psk(b)], writes=[("qB", blk)])
                b = bank("mm")
                mm(b, [(wkv[:, k, :], hT[:, k, c0:c1]) for k in range(KD)], reads=[wkkk] + hk)
                act(kA[0:64, c0:c1], PS[b][0:64, :], AF.Copy, reads=[psk(b)], writes=[("kA", blk)])
                dve(lambda e, b=b, c0=c0, c1=c1: e.tensor_copy(out=kB[0:64, c0:c1], in_=PS[b][64:128, :]),
                    reads=[psk(b)], writes=[("kB", blk)])
            for r, (qt_, kt_, qn, kn) in enumerate(((qA, kA, "qA", "kA"), (qB, kB, "qB", "kB"))):
                h = 2 * j + r
                for qb in range(NB):
                    ab = bank("acc")
                    nkb = 4 * (qb + 1)
                    for kb in range(nkb):
                        jj = kb - 4 * qb
                        cc = jj * 128 if jj > 0 else 0
                        bs = bank("mm")
                        pairs = [(kt_[0:70, kb * 128:(kb + 1) * 128], qt_[0:70, qb * 512 + cc:(qb + 1) * 512], (cc, 512))]
                        rd = [(kn, kb // 4), (kn, "aug"), (qn, qb), (qn, "aug")]
                        if jj >= 0:
                            pairs.append((ident[:, :], negA[:, 0, :], (cc, cc + 128)))
                            rd += ["ident", "negA"]
                        mm(bs, pairs, reads=rd)
                        pT = pbuf[pi % 3]
                        pkk = ("pT", pi % 3)
                        pi += 1
                        act(pT[:, cc:512], PS[bs][:, cc:512], AF.Exp, reads=[psk(bs)], writes=[pkk])

                        def pv(e, ab=ab, kb=kb, h=h, pT=pT, cc=cc, nkb=nkb):
                            return e.matmul(PS[ab][:, cc:512], lhsT=Vf[:, kb, h, :], rhs=pT[:, cc:512],
                                            start=(kb == 0), stop=(kb == nkb - 1))
                        S.op("pe", pv, reads=[pkk, ("Vf", kb, h // 4), "Vf1"], writes=[psk(ab)], cost=(512 - cc) / 2.4 + 12.0)
                    dve(lambda e, ab=ab: e.reciprocal(out=rden[:, :], in_=PS[ab][64:128, :]),
                        reads=[psk(ab)], writes=["rden"])
                    dve(lambda e, ab=ab, r=r, j=j, qb=qb: e.tensor_tensor(
                        out=o_foxT[r * 64:(r + 1) * 64, j, qb * 512:(qb + 1) * 512], in0=PS[ab][0:64, :],
                        in1=rden[:, :], op=ALU.mult), reads=[psk(ab), "rden"], writes=[("ofox", j, qb, r)])
        phase([(l, ("br", 0, 0)), (l, ("gate", 0, 0)), (l, ("br", 0, 1))])
        a = LOC + 4 * S_len * 2
        sgt, tmpb = [], []
        for i in range(2):
            t_, a = alloc("sgt", [128, 512], F32, a)
            sgt.append(t_)
            t_, a = alloc("mtmp", [128, 512], F32, a)
            tmpb.append(t_)
        for blk in range(NB):
            merge_block(l, 0, lambda k, blk=blk: o_foxT[:, k, blk * 512:(blk + 1) * 512],
                        [("ofox", k, blk, r) for k in range(4) for r in range(2)], 4, mergedT, blk,
                        (sgt, tmpb), True)

        phase([(l, ("sk", 0)), (l, ("sv", 0)), (l, ("sq", 0))])
        a = LOC
        sgt, tmpb = [], []
        for i in range(2):
            t_, a = alloc("sgt", [128, 512], F32, a)
            sgt.append(t_)
            t_, a = alloc("mtmp", [128, 512], F32, a)
            tmpb.append(t_)
        kT2, a = alloc("kT2", [128, S_len], BF16, a)
        vg, a = alloc("vg", [128, T, 2, 128], BF16, a)
        qg2 = []
        for i in range(2):
            t_, a = alloc("qg2", [128, 4, 4, 128], BF16, a)
            qg2.append(t_)
        pO, pP = [], []
        for i in range(2):
            t_, a = alloc("pO", [128, 512], BF16, a)
            pO.append(t_)
            t_, a = alloc("pP", [128, 512], BF16, a)
            pP.append(t_)
        skt, a = alloc("skt", [64, 8], F32, a)
        sexp, a = alloc("sexp", [64, 8, 128], F32, a)
        dtot, a = alloc("dtot", [64, 512], F32, a)
        rdn, a = alloc("rdn", [64, 512], F32, a)
        oswa = []
        for i in range(2):
            t_, a = alloc("oswa", [128, 4, 512], BF16, a)
            oswa.append(t_)

        wt, wk = load_w((l, ("sk", 0)))
        wv = wt[:, 0:1024].rearrange("p (k c) -> p k c", k=KD)
        for blk in range(NB):
            c0, c1 = blk * 512, (blk + 1) * 512
            b = bank("mm")
            mm(b, [(wv[:, k, :], hT[:, k, c0:c1]) for k in range(KD)], reads=[wk] + [("hT", blk * 4 + i) for i in range(4)])
            act(kT2[:, c0:c1], PS[b][:, :], AF.Copy, reads=[psk(b)], writes=[("kT2", blk)])
        dve(lambda e: e.memset(vg[:, :, :, 64:128], 1.0), reads=[], writes=["vg1"])
        wt, wk = load_w((l, ("sv", 0)))
        wv = wt[:, 0:1024].rearrange("p (k c) -> p k c", k=KD)
        for t in range(T):
            b = bank("mm")
            mm(b, [(hT[:, k, t * 128:(t + 1) * 128], wv[:, k, :]) for k in range(KD)], reads=[wk, ("hT", t)],
               cols=(0, 128))
            act(vg[:, t, :, 0:64], PS[b][:, 0:128].rearrange("p (g d) -> p g d", g=2), AF.Copy,
                reads=[psk(b)], writes=[("vg", t)])
        sp_dma(skt[:, :], sk_d[l:l + 1, :].broadcast_to([64, 8]), writes=["skt"])
        act(sexp[:, :, :], skt[:, :].unsqueeze(2).broadcast_to([64, 8, 128]), AF.Exp, reads=["skt"], writes=["sexp"])
        pj = 0
        for blk in range(NB):
            c0, c1 = blk * 512, (blk + 1) * 512
            hk = [("hT", blk * 4 + i) for i in range(4)]
            qg = qg2[blk % 2]
            qgk = ("qg2", blk % 2)
            for j in range(4):
                wt, wk = load_w((l, ("sq", j)))
                wv = wt[:, 0:1024].rearrange("p (k c) -> p k c", k=KD)
                b = bank("mm")
                mm(b, [(wv[:, k, :], hT[:, k, c0:c1]) for k in range(KD)], reads=[wk] + hk)
                for r in range(2):
                    h = 2 * j + r
                    g, hh = h // 4, h % 4
                    src = PS[b][r * 64:(r + 1) * 64, :].rearrange("p (n t) -> p n t", n=4)
                    dst = qg[g * 64:(g + 1) * 64, :, hh, :]
                    if r == 0:
                        act(dst, src, AF.Copy, reads=[psk(b)], writes=[(qgk, h)], scale=0.125)
                    else:
                        dve(lambda e, dst=dst, src=src: e.tensor_scalar(out=dst, in0=src, scalar1=0.125, scalar2=None,
                                                                          op0=ALU.mult), reads=[psk(b)], writes=[(qgk, h)])
            ot = oswa[blk % 2]
            for nl in range(4):
                n = blk * 4 + nl
                for g in range(2):
                    rhs = qg[g * 64:(g + 1) * 64, nl, :, :]
                    qr = [(qgk, g * 4 + hh) for hh in range(4)]
                    bo = bank("mm")
                    mm(bo, [(kT2[g * 64:(g + 1) * 64, n * 128:(n + 1) * 128], rhs), (ident[:, :], negA[:, :, :])],
                       reads=[("kT2", n // 4), "ident", "negA"] + qr)
                    po = pO[pj % 2]
                    act(po[:, :], PS[bo][:, :], AF.Exp, reads=[psk(bo)], writes=[("pO", pj % 2)])
                    pvp = []
                    pvr = []
                    if n > 0:
                        bp = bank("mm")
                        mm(bp, [(kT2[g * 64:(g + 1) * 64, (n - 1) * 128:n * 128], rhs), (ident[:, :], negB[:, :, :])],
                           reads=[("kT2", (n - 1) // 4), "ident", "negB"] + qr)
                        pp = pP[pj % 2]
                        act(pp[:, :], PS[bp][:, :], AF.Exp, reads=[psk(bp)], writes=[("pP", pj % 2)])
                        pvp.append((vg[:, n - 1, g, :], pp[:, :]))
                        pvr += [("pP", pj % 2), ("vg", n - 1)]
                    pvp.append((vg[:, n, g, :], po[:, :]))
                    pvr += [("pO", pj % 2), ("vg", n), "vg1"]
                    pj += 1
                    ab = bank("acc")
                    mm(ab, pvp, reads=pvr)
                    dve(lambda e, ab=ab, g=g: e.tensor_tensor(
                        out=dtot[:, :], in0=PS[ab][64:128, :],
                        in1=sexp[:, 4 * g:4 * g + 4, :].rearrange("p h t -> p (h t)"), op=ALU.add),
                        reads=[psk(ab), "sexp"], writes=["dtot"])
                    dve(lambda e: e.reciprocal(out=rdn[:, :], in_=dtot[:, :]), reads=["dtot"], writes=["rdn"])
                    for par in range(2):
                        src = PS[ab][0:64, :].rearrange("p (h t) -> p h t", h=4)
                        rv = rdn[:, :].rearrange("p (h t) -> p h t", h=4)

                        def nrm(e, ot=ot, par=par, g=g, nl=nl, src=src, rv=rv):
                            ins = None
                            for c in range(2):
                                hh = 2 * c + par
                                ins = e.tensor_tensor(out=ot[par * 64:(par + 1) * 64, 2 * g + c, nl * 128:(nl + 1) * 128],
                                                      in0=src[:, hh, :], in1=rv[:, hh, :], op=ALU.mult)
                            return ins
                        dve(nrm, reads=[psk(ab), "rdn"], writes=[("oswa", blk % 2, g, nl, par)])
            ok = [("oswa", blk % 2, g, nl, par) for g in range(2) for nl in range(4) for par in range(2)]
            merge_block(l, 1, lambda k, ot=ot: ot[:, k, :], ok, 4, mergedT, blk, (sgt, tmpb), False)

        phase([(l, ("glr", 0)), (l, ("gq", 0)), (l, ("gk", 0))])
        set_pools("wide")
        a = LOC
        sgt, tmpb = [], []
        for i in range(2):
            t_, a = alloc("sgt", [128, 512], BF16, a)
            sgt.append(t_)
        t_, a = alloc("mtmp", [128, 512], F32, a)
        tmpb = [t_, t_]
        gbn, a = alloc("gbn", [128, 4], F32, a)
        gnb, a = alloc("gnb", [128, 256], F32, a)
        w2t, a = alloc("w2t", [16, 512], BF16, a)
        glrT, a = alloc("glrT", [16, 512], BF16, a)
        nla, a = alloc("nla", [128, 512], F32, a)
        ge1 = nla
        bn, a = alloc("bn", [128, 512], F32, a)
        eb, a = alloc("eb", [128, 512], F32, a)
        enb, a = alloc("enb", [128, 512], F32, a)
        dec, a = alloc("dec", [128, 4], F32, a)
        qtb, ktb, ktmb, vhb, gsgb, ogb = [], [], [], [], [], []
        for i in range(2):
            t_, a = alloc("qt", [128, 512], BF16, a)
            qtb.append(t_)
            t_, a = alloc("kt", [128, 512], BF16, a)
            ktb.append(t_)
            t_, a = alloc("ktm", [128, 4, 128], BF16, a)
            ktmb.append(t_)
            t_, a = alloc("vh", [128, 4, 256], BF16, a)
            vhb.append(t_)
            t_, a = alloc("gsg", [128, 4, 256], BF16, a)
            gsgb.append(t_)
        t_, a = alloc("og", [128, 4, 256], BF16, a)
        ogb = [t_, t_]
        sgr, a = alloc("sgr", [128, 256], F32, a)
        ATb = []
        for i in range(2):
            t_, a = alloc("AT", [128, 128], BF16, a)
            ATb.append(t_)
        Sst, a = alloc("Sst", [128, 4, 256], F32, a)
        Rt, a = alloc("Rt", [128, 256], F32, a)
        Sbf, a = alloc("Sbf", [128, 4, 256], BF16, a)
        gss, a = alloc("gss", [128, 1], F32, a)
        grs, a = alloc("grs", [128, 1], F32, a)
        gjunk, a = alloc("gjunk", [128, 256], BF16, a)
        oglaT, a = alloc("oglaT", [128, 8, 512], BF16, a)

        sp_dma(gbn[:, :], gb_d[l, :, :], writes=["gbn"])
        act(gbn[:, :], gbn[:, :], AF.Copy, reads=["gbn"], writes=["gbn"], scale=-1.0)
        sp_dma(gnb[:, :], gn_d[l:l + 1, :].broadcast_to([128, 256]), writes=["gnb"])
        load_w((l, ("w2", 0)), dst=w2t[:, :], dkey="w2t", rows=16)
        dve(lambda e: e.memset(Sst[:, :, :], 0.0), reads=[], writes=[("Sst", h) for h in range(4)])
        dve(lambda e: e.memset(Sbf[:, :, :], 0.0), reads=[], writes=[("Sbf", h) for h in range(4)])
        gscale = 128.0 ** -0.5
        ai = 0
        for blk in range(NB):
            c0, c1 = blk * 512, (blk + 1) * 512
            hk = [("hT", blk * 4 + i) for i in range(4)]
            wt, wk = load_w((l, ("glr", 0)))
            wv = wt[:, 0:128].rearrange("p (k c) -> p k c", k=KD)
            b = bank("mm")
            mm(b, [(wv[:, k, :], hT[:, k, c0:c1]) for k in range(KD)], reads=[wk] + hk, rows=16)
            act(glrT[:, :], PS[b][0:16, :], AF.Copy, reads=[psk(b)], writes=["glrT"])
            for h in range(4):
                i2 = h % 2
                qt, kt, ktm, vh, gsg, og = qtb[i2], ktb[i2], ktmb[i2], vhb[i2], gsgb[i2], ogb[i2]
                wq, wqk = load_w((l, ("gq", h)))
                wkk, wkkk = load_w((l, ("gk", h)))
                wqv = wq[:, 0:1024].rearrange("p (k c) -> p k c", k=KD)
                wkv = wkk[:, 0:1024].rearrange("p (k c) -> p k c", k=KD)
                bq, bk, bz = bank("mm"), bank("mm"), bank("mm")
                mm(bq, [(wqv[:, k, :], hT[:, k, c0:c1]) for k in range(KD)], reads=[wqk] + hk)
                mm(bk, [(wkv[:, k, :], hT[:, k, c0:c1]) for k in range(KD)], reads=[wkkk] + hk)
                mm(bz, [(w2t[0:16, h * 128:(h + 1) * 128], glrT[0:16, :])], reads=["w2t", "glrT"])
                act(ge1[:, :], PS[bz][:, :], AF.Exp, reads=[psk(bz), "gbn"], writes=["nla"], bias=gbn[:, h:h + 1], scale=-1.0)
                act(nla[:, :], ge1[:, :], AF.Ln, reads=["nla"], writes=["nla"], bias=1.0)
                dve(lambda e: e.tensor_tensor_scan(out=bn[:, :], data0=cmask[:, :], data1=nla[:, :], initial=0.0,
                                                   op0=ALU.mult, op1=ALU.add), reads=["nla", "cmask"], writes=["bn"])
                act(eb[:, :], bn[:, :], AF.Exp, reads=["bn"], writes=["eb"], scale=-1.0 / 16)
                act(enb[:, :], bn[:, :], AF.Exp, reads=["bn"], writes=["enb"], scale=1.0 / 16)
                act(dec[:, :], bn[:, :].rearrange("p (c j) -> p c j", j=128)[:, :, 127], AF.Exp,
                    reads=["bn"], writes=["dec"], scale=-1.0 / 16)
                dve(lambda e, bq=bq, qt=qt: e.scalar_tensor_tensor(out=qt[:, :], in0=PS[bq][:, :], scalar=gscale,
                                                                    in1=eb[:, :], op0=ALU.mult, op1=ALU.mult),
                    reads=[psk(bq), "eb"], writes=[("qt", i2)])
                dve(lambda e, bk=bk, kt=kt: e.tensor_tensor(out=kt[:, :], in0=PS[bk][:, :], in1=enb[:, :], op=ALU.mult),
                    reads=[psk(bk), "enb"], writes=[("kt", i2)])
                bt = bank("aux")
                pst = PS[bt].bitcast(BF16)

                def trk(e, kt=kt, pst=pst):
                    ins = None
                    for c in range(4):
                        ins = e.transpose(out=pst[:, c * 128:(c + 1) * 128], in_=kt[:, c * 128:(c + 1) * 128],
                                          identity=ident[:, :])
                    return ins
                S.op("pe", trk, reads=[("kt", i2), "ident"], writes=[psk(bt)], cost=280.0)
                act(ktm[:, :, :], pst[:, 0:512].rearrange("p (c j) -> p c j", c=4), AF.Copy,
                    reads=[psk(bt)], writes=[("ktm", i2)])
                wt, wk = load_w((l, ("gv", h)))
                wv = wt[:, :].rearrange("p (k c) -> p k c", k=KD)
                for c in range(4):
                    t = blk * 4 + c
                    b = bank("mm")
                    mm(b, [(hT[:, k, t * 128:(t + 1) * 128], wv[:, k, :]) for k in range(KD)], reads=[wk, ("hT", t)],
                       cols=(0, 256))
                    dve(lambda e, vh=vh, c=c, b=b: e.tensor_copy(out=vh[:, c, :], in_=PS[b][:, 0:256]), reads=[psk(b)], writes=[("vh", i2, c)], n=256)
                wt, wk = load_w((l, ("gr", h)))
                wv = wt[:, :].rearrange("p (k c) -> p k c", k=KD)
                for c in range(4):
                    t = blk * 4 + c
                    b = bank("mm")
                    mm(b, [(hT[:, k, t * 128:(t + 1) * 128], wv[:, k, :]) for k in range(KD)], reads=[wk, ("hT", t)],
                       cols=(0, 256))
                    act(sgr[:, :], PS[b][:, 0:256], AF.Silu, reads=[psk(b)], writes=["sgr"])
                    dve(lambda e, gsg=gsg, c=c: e.tensor_tensor(out=gsg[:, c, :], in0=sgr[:, :], in1=gnb[:, :], op=ALU.mult),
                        reads=["sgr", "gnb"], writes=[("gsg", i2, c)])
                for c in range(4):
                    cs = slice(c * 128, (c + 1) * 128)
                    bA = bank("aux")
                    mm(bA, [(kt[:, cs], qt[:, cs])], reads=[("kt", i2), ("qt", i2)], cols=(0, 128))
                    AT = ATb[ai % 2]
                    atk = ("AT", ai % 2)
                    ai += 1
                    dve(lambda e, AT=AT, bA=bA: e.tensor_tensor(out=AT[:, :], in0=PS[bA][:, 0:128], in1=m01[:, :], op=ALU.mult),
                        reads=[psk(bA), "m01"], writes=[atk])
                    bo = bank("mm")
                    mm(bo, [(AT[:, :], vh[:, c, :]), (qt[:, cs], Sbf[:, h, :])],
                       reads=[atk, ("vh", i2, c), ("qt", i2), ("Sbf", h)], cols=(0, 256))
                    bkv = bank("mm")
                    mm(bkv, [(ktm[:, c, :], vh[:, c, :])], reads=[("ktm", i2), ("vh", i2, c)], cols=(0, 256))
                    act(gjunk[:, :], PS[bo][:, 0:256], AF.Square, reads=[psk(bo)], writes=["gjunk", "gss"], accum=gss[:, 0:1])
                    act(grs[:, :], gss[:, :], AF.Ln, reads=["gss", "epsT"], writes=["grs"], bias=epsT[:, 0:1], scale=1.0 / 256)
                    act(grs[:, :], grs[:, :], AF.Exp, reads=["grs"], writes=["grs"], scale=-0.5)
                    dve(lambda e, og=og, bo=bo, c=c, gsg=gsg: e.scalar_tensor_tensor(
                        out=og[:, c, :], in0=PS[bo][:, 0:256], scalar=grs[:, 0:1], in1=gsg[:, c, :],
                        op0=ALU.mult, op1=ALU.mult), reads=[psk(bo), "grs", ("gsg", i2, c)], writes=[("og", 0, c)])
                    dve(lambda e, bkv=bkv, h=h: e.tensor_tensor(out=Rt[:, :], in0=PS[bkv][:, 0:256], in1=Sst[:, h, :], op=ALU.add),
                        reads=[psk(bkv), ("Sst", h)], writes=["Rt"])
                    dve(lambda e, h=h, c=c: e.tensor_scalar(out=Sst[:, h, :], in0=Rt[:, :], scalar1=dec[:, c:c + 1],
                                                            scalar2=None, op0=ALU.mult),
                        reads=["Rt", "dec"], writes=[("Sst", h)])
                    dve(lambda e, h=h, c=c: e.tensor_scalar(out=Sbf[:, h, :], in0=Rt[:, :], scalar1=dec[:, c:c + 1],
                                                            scalar2=None, op0=ALU.mult),
                        reads=["Rt", "dec"], writes=[("Sbf", h)], n=256)
                bt = bank("aux")
                pst = PS[bt].bitcast(BF16)

                def tro(e, og=og, pst=pst):
                    ins = None
                    for hf in range(2):
                        for c in range(4):
                            ins = e.transpose(out=pst[:, (hf * 4 + c) * 128:(hf * 4 + c + 1) * 128],
                                              in_=og[:, c, hf * 128:(hf + 1) * 128], identity=ident[:, :])
                    return ins
                S.op("pe", tro, reads=[("og", 0, c) for c in range(4)] + ["ident"], writes=[psk(bt)], cost=560.0)
                dve(lambda e, h=h, pst=pst: e.tensor_copy(out=oglaT[:, 2 * h:2 * h + 2, :],
                                                           in_=pst[:, :].rearrange("p (f t) -> p f t", f=2)),
                    reads=[psk(bt)], writes=[("ogla", h)], n=1024)
            merge_block(l, 2, lambda k: oglaT[:, k, :], [("ogla", h) for h in range(4)], 8, mergedT, blk,
                        (sgt, tmpb), False)

        S.barrier()
        a = LOC
        wo, a = alloc("wo", [128, KD, D], BF16, a)
        for j in range(4):
            load_w((l, ("wo", j)), dst=wo[:, 2 * j:2 * j + 2, :], dkey=("wo", j))
        for t in range(T):
            for dh in range(2):
                b = bank("mm")
                mm(b, [(mergedT[:, k, t * 128:(t + 1) * 128], wo[:, k, dh * 512:(dh + 1) * 512]) for k in range(KD)],
                   reads=[("mg", k, t // 4) for k in range(KD)] + [("wo", j) for j in range(4)])
                dve(lambda e, b=b, t=t, dh=dh: e.tensor_tensor(
                    out=X[:, t, dh * 512:(dh + 1) * 512], in0=PS[b][:, :], in1=X[:, t, dh * 512:(dh + 1) * 512],
                    op=ALU.add), reads=[psk(b), ("x", t, dh)], writes=[("x", t, dh)])

    for l in range(depth):
        ffn(l, 1)
        mixer(l)
        ffn(l, 2)

    S.barrier()
    a = ARENA
    junk, a = alloc("junk", [128, D], BF16, a)
    yb = []
    for i in range(2):
        t_, a = alloc("yb", [128, D], F32, a)
        yb.append(t_)
    sp_dma(wb[:, :], nrm_d[3 * depth:3 * depth + 1, :].broadcast_to([128, D]), writes=["wb"])
    for t in range(T):
        act(junk[:, :], X[:, t, :], AF.Square, reads=xk(t), writes=["junk", ("ss", t)], accum=ssum[:, t:t + 1])
    act(rstd[:, 0:T], ssum[:, 0:T], AF.Sqrt, reads=[("ss", t) for t in range(T)] + ["epsT"], writes=["rstd"],
        bias=epsT[:, 0:1], scale=1.0 / D)
    dve(lambda e: e.reciprocal(out=rstd[:, 0:T], in_=rstd[:, 0:T]), reads=["rstd"], writes=["rstd"])
    stores = []
    for t in range(T):
        y = yb[t % 2]
        dve(lambda e, y=y, t=t: e.scalar_tensor_tensor(out=y[:, :], in0=X[:, t, :], scalar=rstd[:, t:t + 1],
                                                        in1=wb[:, :], op0=ALU.mult, op1=ALU.mult),
            reads=xk(t) + ["rstd", "wb"], writes=[("yb", t % 2)])
        stores.append(sp_dma(y_d[t * 128:(t + 1) * 128, :], y[:, :], reads=[("yb", t % 2)], writes=[("y", t)]))
    fin = S.op("sp", lambda e: e.nop(), reads=[], writes=[])
    for s_ in stores:
        fin.deps.add(s_)
    fin2 = S.op("sp", lambda e: e.nop(), reads=[], writes=[])
    fin2.deps.add(fin)

    S.finalize()
    with contextlib.ExitStack() as es:
        sems = {k: es.enter_context(nc.semaphore("s_" + "_".join(map(str, k)))) for k in S.sem_keys()}
        block = es.enter_context(nc.Block())
        S.emit(block, sems)
    return nc, S


def pack_inputs(inputs, depth):
    plan = weight_plan(depth)
    W = np.concatenate([b(inputs) for _, b, _ in plan], axis=1).astype(np.float32)
    nrm = []
    for l in range(depth):
        nrm += [inputs["ffn1_norm"][l], inputs["mix_norm"][l], inputs["ffn2_norm"][l]]
    nrm.append(inputs["final_norm"])
    nrm = np.ascontiguousarray(np.stack(nrm, 0), dtype=np.float32)
    fb = np.ascontiguousarray(inputs["fox_forget_bias"][:depth].reshape(depth, 8, 1), dtype=np.float32)
    sk = np.ascontiguousarray(inputs["swa_sinks"][:depth], dtype=np.float32)
    gb = np.ascontiguousarray(inputs["gla_gate_bias"][:depth].reshape(depth, 4, 128).transpose(0, 2, 1), dtype=np.float32)
    gn = np.ascontiguousarray(inputs["gla_norm"][:depth], dtype=np.float32)
    return {"W": W, "nrm": nrm, "fbias": fb, "sinks": sk, "gbias": gb, "gnorm": gn}


_CACHE = {}


def run(inputs, depth=4):
    inputs = {k: np.asarray(v) for k, v in inputs.items()}
    x = inputs["x"]
    B, S_len, _ = x.shape
    key = (S_len, depth)
    if key not in _CACHE:
        _CACHE[key] = build_program(S_len, depth)[0]
    nc = _CACHE[key]
    shared = pack_inputs(inputs, depth)
    in_maps = []
    for b in range(B):
        m = dict(shared)
        m["x"] = np.ascontiguousarray(x[b], dtype=np.float32)
        in_maps.append(m)
    res = run_bass_kernel_spmd(nc, in_maps, core_ids=list(range(B)))
    return np.stack([res.results[b]["y"] for b in range(B)], 0).astype(np.float32)


def kernel(**inputs):
    return run(inputs, depth=4)
```
